# Optimizing a Trainium2 kernel written in Bass

```python
import math
import jax, jax.numpy as jnp
from jax import lax
import numpy as np

D_MODEL = 1024
BATCH = 8
SEQ = 4096
DEPTH = 1

ATTN_HEADS = 8
ATTN_HEAD_DIM = 64
ATTN_V_DIM = 2 * ATTN_HEAD_DIM
QK_WIDTH = ATTN_HEADS * 2 * ATTN_HEAD_DIM
ATTN_WIDTH = ATTN_HEADS * ATTN_V_DIM
ROPE_THETA = 500000.0
ROPE_DIM = ATTN_HEAD_DIM // 4
Q_BLOCK = 128
SUBLN_EPS = 1e-5
RNN_WIDTH = 1024
RNN_BLOCKS = 8
RNN_BLOCK_DIM = RNN_WIDTH // RNN_BLOCKS
CONV_WIDTH = 4
LRU_C = 8.0
LN_EPS = 1e-5

IN_WIDTHS = (QK_WIDTH, QK_WIDTH, ATTN_WIDTH, ATTN_WIDTH, RNN_WIDTH, RNN_WIDTH, D_MODEL, D_MODEL)
IN_WIDTH = sum(IN_WIDTHS)
IN_SPLITS = tuple(int(s) for s in np.cumsum(IN_WIDTHS)[:-1])

kernel_name = "hybrid_diffattn_rglru_gated_deepnorm"


def _rope_partial(t, pos):
    inv = ROPE_THETA ** (-jnp.arange(0, ROPE_DIM, 2, dtype=jnp.float32) / ROPE_DIM)
    ang = pos.astype(jnp.float32)[:, None] * inv[None, :]
    cos = jnp.concatenate([jnp.cos(ang), jnp.cos(ang)], -1)[None, :, None, None, :]
    sin = jnp.concatenate([jnp.sin(ang), jnp.sin(ang)], -1)[None, :, None, None, :]
    tr, tp = t[..., :ROPE_DIM], t[..., ROPE_DIM:]
    half = ROPE_DIM // 2
    rot = jnp.concatenate([-tr[..., half:], tr[..., :half]], -1)
    tr = (tr.astype(jnp.float32) * cos + rot.astype(jnp.float32) * sin).astype(t.dtype)
    return jnp.concatenate([tr, tp], -1)


def _diff_attention(q, k, v, lam):
    B, S = q.shape[0], q.shape[1]
    nb = S // Q_BLOCK
    scale = ATTN_HEAD_DIM ** -0.5
    kh = jnp.transpose(k, (0, 2, 3, 1, 4))
    vh = jnp.transpose(v, (0, 2, 1, 3))
    qb = q.reshape(B, nb, Q_BLOCK, ATTN_HEADS, 2, ATTN_HEAD_DIM).transpose(1, 0, 3, 4, 2, 5)
    k_pos = jnp.arange(S)

    def block(args):
        qi, i = args
        s = jnp.einsum('bhmqd,bhmkd->bhmqk', qi, kh).astype(jnp.float32) * scale
        q_pos = i * Q_BLOCK + jnp.arange(Q_BLOCK)
        mask = k_pos[None, :] <= q_pos[:, None]
        p = jax.nn.softmax(jnp.where(mask, s, -jnp.inf), axis=-1)
        a = p[:, :, 0] - lam * p[:, :, 1]
        return jnp.einsum('bhqk,bhkv->bqhv', a.astype(vh.dtype), vh)

    out = lax.map(block, (qb, jnp.arange(nb)))
    return out.transpose(1, 0, 2, 3, 4).reshape(B, S, ATTN_HEADS, ATTN_V_DIM)


def _causal_conv(x, w, b):
    S = x.shape[1]
    xp = jnp.pad(x, ((0, 0), (CONV_WIDTH - 1, 0), (0, 0)))
    out = b
    for j in range(CONV_WIDTH):
        out = out + w[j] * xp[:, j:j + S]
    return out


def _rg_lru(x, w_a, b_a, w_x, b_x, lam_p):
    B, S, C = x.shape
    xb = x.reshape(B, S, RNN_BLOCKS, RNN_BLOCK_DIM)
    r = jax.nn.sigmoid((jnp.einsum('bsnc,ncd->bsnd', xb, w_a) + b_a).astype(jnp.float32)).reshape(B, S, C)
    i = jax.nn.sigmoid((jnp.einsum('bsnc,ncd->bsnd', xb, w_x) + b_x).astype(jnp.float32)).reshape(B, S, C)
    log_a = -LRU_C * r * jax.nn.softplus(-lam_p.astype(jnp.float32))
    a = jnp.exp(log_a)
    u = jnp.sqrt(-jnp.expm1(2.0 * log_a)) * i * x.astype(jnp.float32)

    def comb(left, right):
        a1, b1 = left
        a2, b2 = right
        return a1 * a2, a2 * b1 + b2

    _, h = lax.associative_scan(comb, (a, u), axis=1)
    return h.astype(x.dtype)


def setup_inputs(seed: int = 0) -> dict:
    key = jax.random.key(seed)
    ks = jax.random.split(key, 20)
    beta = (8.0 * DEPTH) ** -0.25
    nrm = lambda k, shape, s: jax.random.normal(k, shape, jnp.float32) * s
    x = jax.random.normal(ks[0], (BATCH, SEQ, D_MODEL), jnp.float32)
    w_in = nrm(ks[1], (DEPTH, D_MODEL, IN_WIDTH), D_MODEL ** -0.5)
    v0, v1 = IN_SPLITS[1], IN_SPLITS[2]
    w_in = w_in.at[:, :, v0:v1].multiply(beta)
    lam_q1 = nrm(ks[2], (DEPTH, ATTN_HEAD_DIM), 0.1)
    lam_k1 = nrm(ks[3], (DEPTH, ATTN_HEAD_DIM), 0.1)
    lam_q2 = nrm(ks[4], (DEPTH, ATTN_HEAD_DIM), 0.1)
    lam_k2 = nrm(ks[5], (DEPTH, ATTN_HEAD_DIM), 0.1)
    subln_g = 1.0 + nrm(ks[6], (DEPTH, ATTN_V_DIM), 0.02)
    conv_w = nrm(ks[7], (DEPTH, CONV_WIDTH, RNN_WIDTH), CONV_WIDTH ** -0.5)
    conv_b = nrm(ks[8], (DEPTH, RNN_WIDTH), 0.01)
    w_a = nrm(ks[9], (DEPTH, RNN_BLOCKS, RNN_BLOCK_DIM, RNN_BLOCK_DIM), RNN_BLOCK_DIM ** -0.5)
    b_a = nrm(ks[10], (DEPTH, RNN_BLOCKS, RNN_BLOCK_DIM), 0.01)
    w_x = nrm(ks[11], (DEPTH, RNN_BLOCKS, RNN_BLOCK_DIM, RNN_BLOCK_DIM), RNN_BLOCK_DIM ** -0.5)
    b_x = nrm(ks[12], (DEPTH, RNN_BLOCKS, RNN_BLOCK_DIM), 0.01)
    a_c = jax.random.uniform(ks[13], (DEPTH, RNN_WIDTH), jnp.float32, 0.9, 0.999)
    a0 = a_c ** (1.0 / LRU_C)
    lru_lambda = jnp.log(a0) - jnp.log1p(-a0)
    merge_b = nrm(ks[14], (DEPTH, 2 * D_MODEL), 0.01)
    w_attn_proj = nrm(ks[15], (DEPTH, ATTN_WIDTH, D_MODEL), ATTN_WIDTH ** -0.5 * beta)
    w_rnn_proj = nrm(ks[16], (DEPTH, RNN_WIDTH, D_MODEL), RNN_WIDTH ** -0.5 * beta)
    w_out = nrm(ks[17], (DEPTH, D_MODEL, D_MODEL), D_MODEL ** -0.5 * beta)
    ln_g = 1.0 + nrm(ks[18], (DEPTH, D_MODEL), 0.02)
    ln_b = nrm(ks[19], (DEPTH, D_MODEL), 0.01)
    return {"x": x, "w_in": w_in, "lam_q1": lam_q1, "lam_k1": lam_k1, "lam_q2": lam_q2,
            "lam_k2": lam_k2, "subln_g": subln_g, "conv_w": conv_w, "conv_b": conv_b,
            "w_a": w_a, "b_a": b_a, "w_x": w_x, "b_x": b_x, "lru_lambda": lru_lambda,
            "merge_b": merge_b, "w_attn_proj": w_attn_proj, "w_rnn_proj": w_rnn_proj,
            "w_out": w_out, "ln_g": ln_g, "ln_b": ln_b}


def reference(x, w_in, lam_q1, lam_k1, lam_q2, lam_k2, subln_g, conv_w, conv_b, w_a, b_a,
              w_x, b_x, lru_lambda, merge_b, w_attn_proj, w_rnn_proj, w_out, ln_g, ln_b):
    B, S, _ = x.shape
    pos = jnp.arange(S)
    alpha = (2.0 * DEPTH) ** 0.25
    for l in range(DEPTH):
        lambda_init = 0.8 - 0.6 * math.exp(-0.3 * l)
        proj = jnp.einsum('bsd,de->bse', x, w_in[l])
        q, k, v, g_att, x_rnn, g_rnn, m_att, m_rnn = jnp.split(proj, IN_SPLITS, axis=-1)

        q = _rope_partial(q.reshape(B, S, ATTN_HEADS, 2, ATTN_HEAD_DIM), pos)
        k = _rope_partial(k.reshape(B, S, ATTN_HEADS, 2, ATTN_HEAD_DIM), pos)
        v = v.reshape(B, S, ATTN_HEADS, ATTN_V_DIM)
        lam = (jnp.exp(jnp.sum(lam_q1[l].astype(jnp.float32) * lam_k1[l].astype(jnp.float32)))
               - jnp.exp(jnp.sum(lam_q2[l].astype(jnp.float32) * lam_k2[l].astype(jnp.float32)))
               + lambda_init)
        o = _diff_attention(q, k, v, lam).astype(jnp.float32)
        o = o * lax.rsqrt(jnp.mean(o * o, axis=-1, keepdims=True) + SUBLN_EPS)
        o = (o * subln_g[l].astype(jnp.float32) * (1.0 - lambda_init)).astype(x.dtype)
        o = o.reshape(B, S, ATTN_WIDTH) * jax.nn.silu(g_att)
        y_att = jnp.einsum('bse,ed->bsd', o, w_attn_proj[l])

        xc = _causal_conv(x_rnn, conv_w[l], conv_b[l])
        h = _rg_lru(xc, w_a[l], b_a[l], w_x[l], b_x[l], lru_lambda[l])
        y_rnn = jnp.einsum('bse,ed->bsd', h * jax.nn.silu(g_rnn), w_rnn_proj[l])

        gates = jax.nn.sigmoid(jnp.concatenate([m_att, m_rnn], -1) + merge_b[l])
        merged = gates[..., :D_MODEL] * y_att + gates[..., D_MODEL:] * y_rnn
        out = jnp.einsum('bsd,de->bse', merged, w_out[l])

        y = (alpha * x + out).astype(jnp.float32)
        mu = jnp.mean(y, axis=-1, keepdims=True)
        var = jnp.mean(jnp.square(y - mu), axis=-1, keepdims=True)
        y = (y - mu) * lax.rsqrt(var + LN_EPS) * ln_g[l].astype(jnp.float32) + ln_b[l].astype(jnp.float32)
        x = y.astype(x.dtype)
    return x
```

```python
import math
from contextlib import ExitStack

import numpy as np
import concourse.bass as bass
import concourse.mybir as mybir
from concourse.bass_utils import run_bass_kernel_spmd

F32 = mybir.dt.float32
BF16 = mybir.dt.bfloat16
ALU = mybir.AluOpType
AF = mybir.ActivationFunctionType
AX = mybir.AxisListType

S_LEN = 4096
D = 1024
NH = 8
NCH = 8
ENGS = ["pe", "act", "dve", "pool", "sp"]


class Op:
    __slots__ = ("eng", "fn", "deps", "needed", "val", "dma_sem")

    def __init__(self, eng, fn, dma_sem):
        self.eng = eng
        self.fn = fn
        self.deps = set()
        self.needed = False
        self.val = 0
        self.dma_sem = dma_sem


class WL(list):
    extra = ()


class Sched:
    def __init__(self):
        self.ops = {e: [] for e in ENGS}
        self.last_w = {}
        self.readers = {}

    def op(self, eng, fn, reads=(), writes=(), dma_sem=None, extra=()):
        o = Op(eng, fn, dma_sem)
        deps = set(extra)
        deps.update(getattr(writes, "extra", ()))
        for k in reads:
            w = self.last_w.get(k)
            if w is not None:
                deps.add(w)
            if k.startswith("ps"):
                rd = self.readers.get(k)
                if rd:
                    deps.update(r for e2, r in rd.items() if e2 != eng)
        for k in writes:
            w = self.last_w.get(k)
            if w is not None:
                if not (dma_sem is not None and w.dma_sem is dma_sem):
                    deps.add(w)
            rd = self.readers.get(k)
            if rd:
                deps.update(rd.values())
        o.deps = deps
        for k in reads:
            rd = self.readers.setdefault(k, {})
            rd[eng if dma_sem is None else (eng, id(o))] = o
        for k in writes:
            self.last_w[k] = o
            self.readers[k] = {}
        self.ops[eng].append(o)
        return o

    def overwrite_deps(self, keys):
        d = set()
        for k in keys:
            w = self.last_w.get(k)
            if w is not None:
                d.add(w)
            rd = self.readers.get(k)
            if rd:
                d.update(rd.values())
        return d

    def emit(self, block, engsem, final_waits=()):
        for e in ENGS:
            for o in self.ops[e]:
                for d in o.deps:
                    d.needed = True
        for o in final_waits:
            o.needed = True
        dma_cnt = {}
        for e in ENGS:
            cnt = 0
            for o in self.ops[e]:
                if o.dma_sem is not None:
                    k = id(o.dma_sem)
                    dma_cnt[k] = dma_cnt.get(k, 0) + 16
                    o.val = dma_cnt[k]
                elif o.needed:
                    cnt += 1
                    o.val = cnt
        self.stats = {e: len(self.ops[e]) for e in ENGS}

        def make_body(ename, is_last):
            def body(e):
                waited = {}
                nw = 0
                for o in self.ops[ename]:
                    for d in sorted(o.deps, key=lambda d: d.val):
                        if d.dma_sem is None and d.eng == "pe" and ename == "pe":
                            continue
                        sem = d.dma_sem if d.dma_sem is not None else engsem[d.eng]
                        key = id(sem)
                        if waited.get(key, 0) >= d.val:
                            continue
                        e.wait_ge(sem, d.val)
                        nw += 1
                        waited[key] = d.val
                    ins = o.fn(e)
                    if o.dma_sem is not None:
                        ins.then_inc(o.dma_sem, 16)
                    elif o.needed:
                        ins.then_inc(engsem[ename], 1)
                if is_last:
                    for o in final_waits:
                        sem = o.dma_sem if o.dma_sem is not None else engsem[o.eng]
                        e.wait_ge(sem, o.val)
                self.stats[ename + "_waits"] = nw
            return body

        block.tensor(make_body("pe", False))
        block.scalar(make_body("act", False))
        block.vector(make_body("dve", False))
        block.gpsimd(make_body("pool", False))
        block.sync(make_body("sp", True))


def build_program(n_heads=NH, phase_b=True, dbg=False, stage=9):
    nc = bass.Bass("TRN2", target_bir_lowering=False)
    S = Sched()

    def din(name, shape, dt=F32):
        return nc.dram_tensor(name, list(shape), dt, kind="ExternalInput").ap()

    xT_d = din("xT", [D, S_LEN])
    x_d = din("x", [S_LEN, D])
    w_in_d = din("w_in", [D, 8192])
    wap_d = din("w_attn_proj", [D, D])
    wrp_d = din("w_rnn_proj", [D, D])
    wo_d = din("w_out", [D, D])
    wa_d = din("w_a", [8, 128, 128])
    wx_d = din("w_x", [8, 128, 128])
    pp_d = din("pp", [128, 96])
    lamv_d = din("lamv", [128, 256])
    lngb_d = din("lngb", [128, 2048])
    rope_d = din("rope", [128, 2048])
    cm_d = din("cm", [128, 384])
    out_d = nc.dram_tensor("out", [S_LEN, D], F32, kind="ExternalOutput").ap()
    on_scr = nc.dram_tensor("on_scr", [128, 8, S_LEN], BF16,
                            kind="ExternalOutput" if dbg else "Internal").ap()

    w_in_v = w_in_d.rearrange("(kc p) n -> p kc n", p=128)
    xT_v = xT_d.rearrange("(kc p) t -> p kc t", p=128)

    with ExitStack() as es:
        def sb(name, shape, dt):
            return es.enter_context(nc.sbuf_tensor("sb_" + name, list(shape), dt))

        R1 = sb("R1", [128, 32768], BF16)
        R2 = sb("R2", [128, 32768], BF16)
        R3 = sb("R3", [128, 19456], BF16)
        R3b = sb("R3b", [128, 2048], F32)
        R4 = sb("R4", [128, 4868], F32)
        tabs = sb("tabs", [128, 2048], F32)
        onblk = sb("onblk", [128, 1024], BF16)
        cm_bf = sb("cm_bf", [128, 384], BF16)
        pp = sb("pp", [128, 96], F32)
        lamv = sb("lamv", [128, 256], F32)
        sm = sb("sm", [128, 64], F32)
        halo = sb("halo", [128, 32], F32)
        hstate = sb("hstate", [128, 8], F32)
        rtmp = sb("rtmp", [128, 128], F32)
        lnst = sb("lnst", [128, 16], F32)
        psb = [es.enter_context(nc.psum_tensor(f"psum{i}", [128, 512], F32)) for i in range(8)]

        def sem(name):
            return es.enter_context(nc.semaphore(name))

        engsem = {e: sem("s_" + e) for e in ENGS}
        d_xT = [sem(f"d_xT{g}") for g in range(4)]
        d_wqkv = [sem(f"d_wqkv{s}") for s in range(2)]
        d_wg = [sem("d_wg0")]
        d_WB = sem("d_WB")
        d_pp = sem("d_pp")
        d_lamv = sem("d_lamv")
        d_rope = sem("d_rope")
        d_cm = sem("d_cm")
        d_on = [sem(f"d_on{s}") for s in range(2)]
        d_onr = [sem(f"d_onr{s}") for s in range(2)]
        d_xcb = [sem(f"d_xcb{s}") for s in range(2)]
        d_pw = sem("d_pw")
        d_wax = sem("d_wax")
        d_ln = sem("d_ln")
        d_xres = [sem(f"d_xres{s}") for s in range(2)]
        d_out = sem("d_out")
        block = es.enter_context(nc.Block())

        xT = R1[:, :].rearrange("p (k t) -> p k t", k=8)
        WB = R2[:, :].rearrange("p (k n) -> p k n", k=8)
        qT = R3[:, 0:4096]
        kT = R3[:, 4096:8192]
        V = R3[:, 8192:12288].rearrange("p (t v) -> p t v", t=32)
        wqkv = [R3[:, 12288 + s * 3072:12288 + (s + 1) * 3072].rearrange("p (k n) -> p k n", k=8)
                for s in range(2)]
        wg = [R3[:, 18432:19456].rearrange("p (k n) -> p k n", k=8)]
        R3bb = R3b[:, :].bitcast(BF16)
        pT = [R3bb[:, s * 512:(s + 1) * 512] for s in range(4)]
        qk_tm = [R3bb[:, 2048 + s * 256:2048 + (s + 1) * 256] for s in range(2)]
        cosb = tabs[:, 0:1024].rearrange("p (t f) -> p t f", t=32)
        sinb = tabs[:, 1024:2048].rearrange("p (t f) -> p t f", t=32)
        wa = R3[:, 0:1024].rearrange("p (n d) -> p n d", n=8)
        wx = R3[:, 1024:2048].rearrange("p (n d) -> p n d", n=8)
        onc = [R3[:, 2048 + s * 4096:2048 + (s + 1) * 4096].rearrange("p (h t) -> p h t", h=8)
               for s in range(2)]
        hsb = R3[:, 10240:14336].rearrange("p (n t) -> p n t", n=8)
        mgb = R3[:, 14336:18432].rearrange("p (n t) -> p n t", n=8)
        xres = [R3b[:, s * 1024:(s + 1) * 1024] for s in range(2)]
        lnG = tabs[:, 0:1024]
        lnB = tabs[:, 1024:2048]
        wap = R1[:, 0:8192].rearrange("p (k n) -> p k n", k=8)
        wrp = R1[:, 8192:16384].rearrange("p (k n) -> p k n", k=8)
        wo = R1[:, 16384:24576].rearrange("p (k n) -> p k n", k=8)
        xcb = [R1[:, 24576 + s * 4096:24576 + (s + 1) * 4096].rearrange("p (k t) -> p k t", k=8)
               for s in range(2)]
        tA = [R4[:, i * 512:(i + 1) * 512] for i in range(7)]
        eg, sg, rL0, rL1, o1, o2, tt_ = tA
        rL = [rL0, rL1]
        oo = [o1, o2]
        sqb = R4[:, 3584:3840].bitcast(BF16)
        on_blk = [onblk[:, s * 512:(s + 1) * 512] for s in range(2)]
        xrw = R4[:, 0:516]
        tB = [R4[:, 516 + i * 512:516 + (i + 1) * 512] for i in range(6)]
        accb, erb, eib, ab, a2b, hb = tB
        egb = erb
        xc_bf = R4[:, 3588:3844].bitcast(BF16)
        ybuf = R4[:, 3844:4868]

        ident = cm_bf[:, 0:128]
        tri = cm_bf[:, 128:256]
        ones = cm_bf[:, 256:384]
        psT = [psb[6][:, :].bitcast(BF16), psb[7][:, :].bitcast(BF16)]

        PHASEA_R3 = ([f"qT{g}" for g in range(8)] + [f"kT{g}" for g in range(8)]
                     + [f"V{g}" for g in range(8)] + ["wqkv0", "wqkv1", "wg0",
                     "pT0", "pT1", "pT2", "pT3", "rope", "qk0", "qk1"])
        PHASEA_R4 = ["eg", "sg", "rL0", "rL1", "o0", "o1", "tt", "sq", "onb0", "onb1"]
        XT_KEYS = [f"xT{g}" for g in range(8)]

        def mm(out, lhsT, rhs, start, stop, reads, writes):
            return S.op("pe", lambda e: e.matmul(out, lhsT=lhsT, rhs=rhs, start=start, stop=stop),
                        reads=reads, writes=writes)

        def tr(out, in_, reads, writes):
            return S.op("pe", lambda e: e.transpose(out=out, in_=in_, identity=ident),
                        reads=list(reads) + ["cm"], writes=writes)

        def act(out, in_, func, reads, writes, bias=None, scale=None):
            kw = {}
            if bias is not None:
                kw["bias"] = bias
            if scale is not None:
                kw["scale"] = scale
            return S.op("act", lambda e: e.activation(out=out, in_=in_, func=func, **kw),
                        reads=reads, writes=writes)

        def tt(eng, out, in0, in1, op, reads, writes):
            return S.op(eng, lambda e: e.tensor_tensor(out=out, in0=in0, in1=in1, op=op),
                        reads=reads, writes=writes)

        def ts(eng, out, in0, s1, s2, op0, op1, reads, writes):
            if s2 is None:
                return S.op(eng, lambda e: e.tensor_scalar(out=out, in0=in0, scalar1=s1, scalar2=None,
                                                           op0=op0), reads=reads, writes=writes)
            return S.op(eng, lambda e: e.tensor_scalar(out=out, in0=in0, scalar1=s1, scalar2=s2,
                                                       op0=op0, op1=op1), reads=reads, writes=writes)

        def stt(eng, out, in0, scalar, in1, op0, op1, reads, writes):
            return S.op(eng, lambda e: e.scalar_tensor_tensor(out=out, in0=in0, scalar=scalar, in1=in1,
                                                              op0=op0, op1=op1),
                        reads=reads, writes=writes)

        def cpy(eng, out, in_, reads, writes):
            if eng == "act":
                return S.op("act", lambda e: e.activation(out=out, in_=in_, func=AF.Copy),
                            reads=reads, writes=writes)
            return S.op(eng, lambda e: e.tensor_copy(out=out, in_=in_), reads=reads, writes=writes)

        def sigm(out, in_, reads, key, first_writes=None, nbias=None):
            fwk = first_writes if first_writes is not None else [key]
            if nbias is None:
                act(out, in_, AF.Exp, reads, fwk, scale=-1.0)
            else:
                act(out, in_, AF.Exp, list(reads) + ["sm"], fwk, bias=nbias, scale=-1.0)
            act(out, out, AF.Ln, [key, "sm"], [key], bias=ONE, scale=1.0)
            act(out, out, AF.Exp, [key], [key], scale=-1.0)

        def recip(out, in_, reads, writes):
            return S.op("dve", lambda e: e.reciprocal(out=out, in_=in_), reads=reads, writes=writes)

        def dma(eng, out, in_, dsem, reads, writes):
            return S.op(eng, lambda e: e.dma_start(out=out, in_=in_), reads=reads, writes=writes,
                        dma_sem=dsem)

        def memset(eng, ap, val, writes):
            return S.op(eng, lambda e: e.memset(ap, val), writes=writes)

        dma("pool", cm_bf[:, :], cm_d, d_cm, [], ["cm"])
        dma("sp", pp[:, :], pp_d, d_pp, [], ["pp"])
        dma("sp", lamv[:, :], lamv_d, d_lamv, [], ["lamv"])
        dma("sp", tabs[:, :], rope_d, d_rope, [], ["rope"])

        def load_head_weights(h):
            s = h % 2
            for kc in range(8):
                for i, c0 in enumerate((h * 128, 1024 + h * 128, 2048 + h * 128)):
                    dma("pool", wqkv[s][:, kc, i * 128:(i + 1) * 128],
                        w_in_d[kc * 128:(kc + 1) * 128, c0:c0 + 128], d_wqkv[s], [], [f"wqkv{s}"])

        def load_wg(h):
            c0 = 3072 + h * 128
            for kc in range(8):
                dma("pool", wg[0][:, kc, :], w_in_d[kc * 128:(kc + 1) * 128, c0:c0 + 128], d_wg[0], [], ["wg0"])

        load_head_weights(0)
        load_wg(0)
        for g in range(4):
            for kc in range(8):
                dma("pool", xT[:, kc, g * 1024:(g + 1) * 1024], xT_d[kc * 128:(kc + 1) * 128, g * 1024:(g + 1) * 1024],
                    d_xT[g], [], [f"xT{2 * g}", f"xT{2 * g + 1}"])

        memset("pool", sm[:, :], 0.0, ["sm"])
        memset("pool", sm[:, 2:3], 1.0, ["sm"])
        memset("pool", sm[:, 3:5], 1e-5, ["sm"])
        memset("pool", halo[:, :], 0.0, ["halo"])
        memset("pool", hstate[:, :], 0.0, ["hstate"])
        ONE = sm[:, 2:3]
        EPS = sm[:, 3:4]
        NEG_LAM = sm[:, 0:1]
        GCOEF = sm[:, 1:2]
        lambda_init = 0.8 - 0.6 * math.exp(-0.3 * 0)
        tt("dve", rtmp[:, 0:64], lamv[:, 0:64], lamv[:, 64:128], ALU.mult, ["lamv"], ["rtmp"])
        S.op("dve", lambda e: e.reduce_sum(out=sm[:, 6:7], in_=rtmp[:, 0:64], axis=AX.X),
             reads=["rtmp"], writes=["sm"])
        tt("dve", rtmp[:, 64:128], lamv[:, 128:192], lamv[:, 192:256], ALU.mult, ["lamv"], ["rtmp"])
        S.op("dve", lambda e: e.reduce_sum(out=sm[:, 7:8], in_=rtmp[:, 64:128], axis=AX.X),
             reads=["rtmp"], writes=["sm"])
        act(sm[:, 6:8], sm[:, 6:8], AF.Exp, ["sm"], ["sm"])
        tt("dve", sm[:, 0:1], sm[:, 7:8], sm[:, 6:7], ALU.subtract, ["sm"], ["sm"])
        ts("dve", sm[:, 0:1], sm[:, 0:1], -lambda_init, None, ALU.add, None, ["sm"], ["sm"])
        ts("dve", sm[:, 1:2], pp[:, 80:81], 1.0 - lambda_init, None, ALU.mult, None, ["pp", "sm"], ["sm"])
        act(sm[:, 8:16], pp[:, 56:64], AF.Exp, ["pp", "sm"], ["sm"], scale=-1.0)
        act(sm[:, 8:16], sm[:, 8:16], AF.Ln, ["sm"], ["sm"], bias=ONE, scale=1.0)
        ts("dve", sm[:, 16:24], sm[:, 8:16], -16.0, None, ALU.mult, None, ["sm"], ["sm"])
        ts("dve", sm[:, 8:16], sm[:, 8:16], -8.0, None, ALU.mult, None, ["sm"], ["sm"])
        ts("dve", sm[:, 24:40], pp[:, 40:56], -1.0, None, ALU.mult, None, ["pp", "sm"], ["sm"])
        ts("dve", sm[:, 40:56], pp[:, 64:80], -1.0, None, ALU.mult, None, ["pp", "sm"], ["sm"])

        blk_ctr = [0]
        onb_ctr = [0]

        def in_proj_head(h):
            s = h % 2
            for g in range(8):
                tb = g % 2
                for j in range(4):
                    t_i = 4 * g + j
                    ba = t_i % 2
                    qs = t_i % 2
                    for kc in range(8):
                        mm(psb[ba][:, 0:384], xT[:, kc, t_i * 128:(t_i + 1) * 128], wqkv[s][:, kc, :],
                           kc == 0, kc == 7, [f"xT{g}", f"wqkv{s}"], [f"ps{ba}"])
                    cpy("act", qk_tm[qs][:, :], psb[ba][:, 0:256], [f"ps{ba}"], [f"qk{qs}"])
                    cpy("act", V[:, t_i, :], psb[ba][:, 256:384], [f"ps{ba}"], [f"V{g}"])
                    pv = psb[ba][:, 0:256].rearrange("p (s d) -> p s d", s=4)
                    A_ = pv[:, :, 0:8]
                    B_ = pv[:, :, 8:16]
                    cv = cosb[:, t_i, :].rearrange("p (s d) -> p s d", s=4)
                    sv = sinb[:, t_i, :].rearrange("p (s d) -> p s d", s=4)
                    r = [rtmp[:, i * 32:(i + 1) * 32].rearrange("p (s d) -> p s d", s=4) for i in range(4)]
                    qv = qk_tm[qs][:, :].rearrange("p (s d) -> p s d", s=4)
                    tt("dve", r[0], A_, cv, ALU.mult, [f"ps{ba}", "rope"], ["r0"])
                    tt("dve", r[1], B_, sv, ALU.mult, [f"ps{ba}", "rope"], ["r1"])
                    tt("dve", r[2], B_, cv, ALU.mult, [f"ps{ba}", "rope"], ["r2"])
                    tt("dve", r[3], A_, sv, ALU.mult, [f"ps{ba}", "rope"], ["r3"])
                    tt("dve", qv[:, :, 0:8], r[0], r[1], ALU.subtract, ["r0", "r1"], [f"qk{qs}"])
                    tt("dve", qv[:, :, 8:16], r[2], r[3], ALU.add, ["r2", "r3"], [f"qk{qs}"])
                    tr(psT[tb][:, j * 128:(j + 1) * 128], qk_tm[qs][:, 0:128], [f"qk{qs}"], [f"ps{6 + tb}"])
                    tr(psT[tb][:, 512 + j * 128:512 + (j + 1) * 128], qk_tm[qs][:, 128:256],
                       [f"qk{qs}"], [f"ps{6 + tb}"])
                cpy("act", qT[:, g * 512:(g + 1) * 512], psT[tb][:, 0:512], [f"ps{6 + tb}"], [f"qT{g}"])
                cpy("dve", kT[:, g * 512:(g + 1) * 512], psT[tb][:, 512:1024], [f"ps{6 + tb}"], [f"kT{g}"])

        def gproj(h, c):
            s = h % 2
            psG = psb[6]
            for kc in range(8):
                mm(psG[:, :], wg[0][:, kc, :], xT[:, kc, c * 512:(c + 1) * 512], kc == 0, kc == 7,
                   ["wg0", f"xT{c}"], ["ps6"])
            sigm(eg, psG[:, :], ["ps6"], "eg")
            tt("dve", sg, psG[:, :], eg, ALU.mult, ["ps6", "eg"], ["sg"])

        def norm_tail(h, c):
            ob = onb_ctr[0] % 2
            onb_ctr[0] += 1
            mm(psb[7][:, :], ones, sqb, True, True, ["cm", "sq"], ["ps7"])
            act(tt_, psb[7][:, :], AF.Ln, ["ps7", "sm"], ["tt"], bias=EPS, scale=1.0 / 128.0)
            act(tt_, tt_, AF.Exp, ["tt"], ["tt"], scale=-0.5)
            tt("dve", tt_, o1, tt_, ALU.mult, ["o0", "tt"], ["tt"])
            stt("dve", on_blk[ob], tt_, GCOEF, sg, ALU.mult, ALU.mult, ["tt", "sm", "sg"], [f"onb{ob}"])
            dma("sp", on_scr[:, h, c * 512:(c + 1) * 512], on_blk[ob], d_on[ob], [f"onb{ob}"], [f"on_scr{ob}"])

        def attention_head(h):
            for c in range(NCH):
                gproj(h, c)
                nk = 4 * c + 4
                for m in range(2):
                    for kt in range(nk):
                        j = kt - 4 * c
                        q0 = max(j, 0) * 128
                        b = blk_ctr[0]
                        blk_ctr[0] += 1
                        sbk = b % 2
                        pslot = b % 4
                        mm(psb[sbk][:, q0:512], kT[m * 64:(m + 1) * 64, kt * 128:(kt + 1) * 128],
                           qT[m * 64:(m + 1) * 64, c * 512 + q0:(c + 1) * 512], True, True,
                           [f"kT{kt // 4}", f"qT{c}"], [f"ps{sbk}"])
                        act(pT[pslot][:, q0:512], psb[sbk][:, q0:512], AF.Exp, [f"ps{sbk}"], [f"pT{pslot}"],
                            scale=0.125)
                        if j >= 0:
                            tt("pool", pT[pslot][:, q0:q0 + 128], pT[pslot][:, q0:q0 + 128], tri, ALU.mult,
                               [f"pT{pslot}", "cm"], [f"pT{pslot}"])
                        mm(psb[2 + m][:, q0:512], V[:, kt, :], pT[pslot][:, q0:512], kt == 0, kt == nk - 1,
                           [f"V{kt // 4}", f"pT{pslot}"], [f"ps{2 + m}"])
                        mm(psb[4 + m][:, q0:512], ones, pT[pslot][:, q0:512], kt == 0, kt == nk - 1,
                           ["cm", f"pT{pslot}"], [f"ps{4 + m}"])
                    act(rL[m], psb[4 + m][:, :], AF.Ln, [f"ps{4 + m}"], [f"rL{m}"])
                    act(rL[m], rL[m], AF.Exp, [f"rL{m}"], [f"rL{m}"], scale=-1.0)
                    tt("dve", oo[m], psb[2 + m][:, :], rL[m], ALU.mult, [f"ps{2 + m}", f"rL{m}"], [f"o{m}"])
                stt("dve", o1, o2, NEG_LAM, o1, ALU.mult, ALU.add, ["o1", "o0", "sm"], ["o0"])
                tt("pool", sqb, o1, o1, ALU.mult, ["o0"], ["sq"])
                norm_tail(h, c)

        for h in range(n_heads):
            if h + 1 < n_heads:
                load_head_weights(h + 1)
            if phase_b:
                for kc in range(8):
                    if min(kc, n_heads - 1) == h:
                        for qq in range(4):
                            dma("pool", WB[:, kc, qq * 1024:(qq + 1) * 1024],
                                w_in_d[kc * 128:(kc + 1) * 128, 4096 + qq * 1024:4096 + (qq + 1) * 1024],
                                d_WB, [], ["WB"])
            if stage >= 1:
                in_proj_head(h)
            if stage >= 2:
                attention_head(h)
            if h + 1 < n_heads:
                load_wg(h + 1)

        final = []
        if phase_b:
            fw_seen = set()

            def fw(key, extra):
                if key in fw_seen:
                    return [key]
                fw_seen.add(key)
                wl = WL([key])
                wl.extra = S.overwrite_deps(extra)
                return wl

            def load_chunk(c):
                xs = c % 2
                for kc in range(8):
                    dma("pool", xcb[xs][:, kc, :], xT_d[kc * 128:(kc + 1) * 128, c * 512:(c + 1) * 512], d_xcb[xs], [],
                        fw(f"xcb{xs}", XT_KEYS))
                for hh in range(8):
                    dma("sp", onc[xs][:, hh, :], on_scr[:, hh, c * 512:(c + 1) * 512], d_onr[xs],
                        ["on_scr0", "on_scr1"], fw(f"onc{xs}", PHASEA_R3))

            def load_xres(t_i):
                rs = t_i % 2
                dma("sp", xres[rs], x_d[t_i * 128:(t_i + 1) * 128, :], d_xres[rs], [],
                    fw(f"xres{rs}", PHASEA_R3))

            for c in range(NCH):
                xs = c % 2
                if c == 0:
                    load_chunk(0)
                    for nn in range(8):
                        dma("pool", wa[:, nn, :], wa_d[nn], d_wax, [], fw("wax", PHASEA_R3))
                        dma("pool", wx[:, nn, :], wx_d[nn], d_wax, [], ["wax"])
                    for (wt, wd) in ((wrp, wrp_d), (wap, wap_d), (wo, wo_d)):
                        for kc in range(8):
                            dma("pool", wt[:, kc, :], wd[kc * 128:(kc + 1) * 128, :], d_pw, [],
                                fw("pw", XT_KEYS))
                    dma("sp", lnG, lngb_d[:, 0:1024], d_ln, [], fw("ln", PHASEA_R3))
                    dma("sp", lnB, lngb_d[:, 1024:2048], d_ln, [], ["ln"])
                    load_xres(0)
                if c + 1 < NCH:
                    load_chunk(c + 1)

                for n in range(8):
                    bx = n % 2
                    bg = 4 + n % 2
                    for kc in range(8):
                        mm(psb[bx][:, :], WB[:, kc, n * 128:(n + 1) * 128], xcb[xs][:, kc, :], kc == 0, kc == 7,
                           ["WB", f"xcb{xs}"], [f"ps{bx}"])
                    for kc in range(8):
                        mm(psb[bg][:, :], WB[:, kc, 1024 + n * 128:1024 + (n + 1) * 128], xcb[xs][:, kc, :],
                           kc == 0, kc == 7, ["WB", f"xcb{xs}"], [f"ps{bg}"])
                    cpy("pool", xrw[:, 0:3], halo[:, n * 4:n * 4 + 3], ["halo"], fw("xrw", PHASEA_R4))
                    cpy("act", xrw[:, 3:515], psb[bx][:, :], [f"ps{bx}"], ["xrw"])
                    cpy("pool", halo[:, n * 4:n * 4 + 3], xrw[:, 512:515], ["xrw"], ["halo"])
                    cw = lambda jj: pp[:, n * 4 + jj:n * 4 + jj + 1]
                    ts("dve", accb, xrw[:, 0:512], cw(0), pp[:, 32 + n:33 + n], ALU.mult, ALU.add,
                       ["xrw", "pp"], fw("acc", PHASEA_R4))
                    stt("dve", accb, xrw[:, 1:513], cw(1), accb, ALU.mult, ALU.add, ["xrw", "pp", "acc"], ["acc"])
                    stt("dve", accb, xrw[:, 2:514], cw(2), accb, ALU.mult, ALU.add, ["xrw", "pp", "acc"], ["acc"])
                    stt("dve", accb, xrw[:, 3:515], cw(3), accb, ALU.mult, ALU.add, ["xrw", "pp", "acc"], ["acc"])
                    cpy("pool", xc_bf, accb, ["acc"], fw("xcbf", PHASEA_R4))
                    mm(psb[2][:, :], wa[:, n, :], xc_bf, True, True, ["wax", "xcbf"], ["ps2"])
                    mm(psb[3][:, :], wx[:, n, :], xc_bf, True, True, ["wax", "xcbf"], ["ps3"])
                    sigm(erb, psb[2][:, :], ["ps2"], "er", fw("er", PHASEA_R4), nbias=sm[:, 24 + n:25 + n])
                    sigm(eib, psb[3][:, :], ["ps3"], "ei", fw("ei", PHASEA_R4), nbias=sm[:, 32 + n:33 + n])
                    act(ab, erb, AF.Exp, ["er", "sm"], fw("a", PHASEA_R4), scale=sm[:, 8 + n:9 + n])
                    act(a2b, erb, AF.Exp, ["er", "sm"], fw("a2", PHASEA_R4), scale=sm[:, 16 + n:17 + n])
                    act(a2b, a2b, AF.Ln, ["a2", "sm"], ["a2"], bias=ONE, scale=-1.0)
                    act(a2b, a2b, AF.Exp, ["a2"], ["a2"], scale=0.5)
                    tt("dve", a2b, a2b, eib, ALU.mult, ["a2", "ei"], ["a2"])
                    tt("dve", a2b, a2b, accb, ALU.mult, ["a2", "acc"], ["a2"])
                    S.op("dve", lambda e, n=n: e.tensor_tensor_scan(out=hb, data0=ab, data1=a2b,
                                                                    initial=hstate[:, n:n + 1],
                                                                    op0=ALU.mult, op1=ALU.add),
                         reads=["a", "a2", "hstate"], writes=fw("h", PHASEA_R4))
                    cpy("pool", hstate[:, n:n + 1], hb[:, 511:512], ["h"], ["hstate"])
                    sigm(egb, psb[bg][:, :], [f"ps{bg}"], "egb", fw("egb", PHASEA_R4))
                    tt("dve", egb, psb[bg][:, :], egb, ALU.mult, [f"ps{bg}", "egb"], ["egb"])
                    tt("dve", hsb[:, n, :], hb, egb, ALU.mult, ["h", "egb"], fw("hsb", PHASEA_R3))

                for dch in range(8):
                    pb = 0 if dch % 2 == 0 else 4
                    dsl = slice(dch * 128, (dch + 1) * 128)
                    for hh in range(8):
                        mm(psb[pb][:, :], wap[:, hh, dsl], onc[xs][:, hh, :], hh == 0, hh == 7,
                           ["pw", f"onc{xs}"], [f"ps{pb}"])
                    for nn in range(8):
                        mm(psb[pb + 1][:, :], wrp[:, nn, dsl], hsb[:, nn, :], nn == 0, nn == 7,
                           ["pw", "hsb"], [f"ps{pb + 1}"])
                    for kc in range(8):
                        mm(psb[pb + 2][:, :], WB[:, kc, 2048 + dch * 128:2048 + (dch + 1) * 128], xcb[xs][:, kc, :],
                           kc == 0, kc == 7, ["WB", f"xcb{xs}"], [f"ps{pb + 2}"])
                    for kc in range(8):
                        mm(psb[pb + 3][:, :], WB[:, kc, 3072 + dch * 128:3072 + (dch + 1) * 128], xcb[xs][:, kc, :],
                           kc == 0, kc == 7, ["WB", f"xcb{xs}"], [f"ps{pb + 3}"])
                    sigm(erb, psb[pb + 2][:, :], [f"ps{pb + 2}"], "er", nbias=sm[:, 40 + dch:41 + dch])
                    sigm(eib, psb[pb + 3][:, :], [f"ps{pb + 3}"], "ei", nbias=sm[:, 48 + dch:49 + dch])
                    tt("dve", erb, psb[pb][:, :], erb, ALU.mult, [f"ps{pb}", "er"], ["er"])
                    tt("dve", eib, psb[pb + 1][:, :], eib, ALU.mult, [f"ps{pb + 1}", "ei"], ["ei"])
                    tt("dve", mgb[:, dch, :], erb, eib, ALU.add, ["er", "ei"], fw("mgb", PHASEA_R3))

                alpha = (2.0 * 1) ** 0.25
                for t4 in range(4):
                    t_i = c * 4 + t4
                    rs = t_i % 2
                    pb = 0 if t4 % 2 == 0 else 2
                    if t_i + 1 < 32:
                        load_xres(t_i + 1)
                    for half in range(2):
                        for dch in range(8):
                            mm(psb[pb + half][:, :], mgb[:, dch, t4 * 128:(t4 + 1) * 128],
                               wo[:, dch, half * 512:(half + 1) * 512], dch == 0, dch == 7,
                               ["mgb", "pw"], [f"ps{pb + half}"])
                    for half in range(2):
                        hs_ = slice(half * 512, (half + 1) * 512)
                        stt("dve", ybuf[:, hs_], xres[rs][:, hs_], alpha, psb[pb + half][:, :], ALU.mult, ALU.add,
                            [f"xres{rs}", f"ps{pb + half}"], fw("y", PHASEA_R4))
                    S.op("dve", lambda e: e.bn_stats(out=lnst[:, 0:6], in_=ybuf[:, 0:512]),
                         reads=["y"], writes=["lnst"])
                    S.op("dve", lambda e: e.bn_stats(out=lnst[:, 6:12], in_=ybuf[:, 512:1024]),
                         reads=["y"], writes=["lnst"])
                    S.op("dve", lambda e: e.bn_aggr(out=lnst[:, 12:14],
                                                    in_=lnst[:, 0:12].rearrange("p (a b) -> p a b", a=2)),
                         reads=["lnst"], writes=["lnst"])
                    act(lnst[:, 14:15], lnst[:, 13:14], AF.Ln, ["lnst", "sm"], ["lnst"], bias=sm[:, 4:5], scale=1.0)
                    act(lnst[:, 14:15], lnst[:, 14:15], AF.Exp, ["lnst"], ["lnst"], scale=-0.5)
                    ts("dve", ybuf, ybuf, lnst[:, 12:13], lnst[:, 14:15], ALU.subtract, ALU.mult,
                       ["y", "lnst"], ["y"])
                    tt("pool", ybuf, ybuf, lnG, ALU.mult, ["y", "ln"], ["y"])
                    tt("pool", ybuf, ybuf, lnB, ALU.add, ["y", "ln"], ["y"])
                    final.append(dma("sp", out_d[t_i * 128:(t_i + 1) * 128, :], ybuf, d_out, ["y"], ["out"]))
        else:
            if "on_scr0" in S.last_w:
                final.append(S.last_w["on_scr0"])
                final.append(S.last_w["on_scr1"])
            else:
                final.append(dma("sp", on_scr[:, 0, 0:512], qT[:, 0:512], d_on[0], ["qT0", "kT7", "V7", "sm"], ["on_scr0"]))

        S.emit(block, engsem, final_waits=final[-2:] if not phase_b else final[-1:])
    return nc, S.stats


_CACHE = {}


def _host_consts():
    inv = (500000.0 ** (-np.arange(0, 16, 2, dtype=np.float32) / np.float32(16))).astype(np.float32)
    pos = np.arange(S_LEN, dtype=np.float32)
    ang = (pos[:, None] * inv[None, :]).astype(np.float32)
    cos = np.cos(ang).astype(np.float32)
    sin = np.sin(ang).astype(np.float32)

    def lay(t):
        t = t.reshape(32, 128, 8).transpose(1, 0, 2)
        return np.ascontiguousarray(np.broadcast_to(t[:, :, None, :], (128, 32, 4, 8))).reshape(128, 1024)

    rope = np.concatenate([lay(cos), lay(sin)], axis=1).astype(np.float32)
    ident = np.eye(128, dtype=np.float32)
    tri = (np.arange(128)[None, :] >= np.arange(128)[:, None]).astype(np.float32)
    ones = np.ones((128, 128), np.float32)
    cm = np.concatenate([ident, tri, ones], axis=1)
    return rope, cm


def _prep_inputs(x, w_in, lam_q1, lam_k1, lam_q2, lam_k2, subln_g, conv_w, conv_b, w_a, b_a,
                 w_x, b_x, lru_lambda, merge_b, w_attn_proj, w_rnn_proj, w_out, ln_g, ln_b):
    f = lambda a: np.ascontiguousarray(np.asarray(a, dtype=np.float32))
    rope, cm = _host_consts()
    pp = np.zeros((128, 96), np.float32)
    cw = f(conv_w)[0]
    pp[:, 0:32] = cw.reshape(4, 8, 128).transpose(2, 1, 0).reshape(128, 32)
    pp[:, 32:40] = f(conv_b)[0].reshape(8, 128).T
    pp[:, 40:48] = f(b_a)[0].T
    pp[:, 48:56] = f(b_x)[0].T
    pp[:, 56:64] = f(lru_lambda)[0].reshape(8, 128).T
    pp[:, 64:80] = f(merge_b)[0].reshape(16, 128).T
    pp[:, 80] = f(subln_g)[0]
    lamv = np.concatenate([f(lam_q1)[0], f(lam_k1)[0], f(lam_q2)[0], f(lam_k2)[0]])[None, :]
    lamv = np.ascontiguousarray(np.broadcast_to(lamv, (128, 256)))
    lngb = np.concatenate([f(ln_g)[0], f(ln_b)[0]])[None, :]
    lngb = np.ascontiguousarray(np.broadcast_to(lngb, (128, 2048)))
    shared = {
        "w_in": f(w_in)[0], "w_attn_proj": f(w_attn_proj)[0], "w_rnn_proj": f(w_rnn_proj)[0],
        "w_out": f(w_out)[0], "w_a": f(w_a)[0], "w_x": f(w_x)[0], "pp": pp, "lamv": lamv,
        "lngb": lngb, "rope": rope, "cm": cm,
    }
    xs = f(x)
    in_maps = []
    for b in range(xs.shape[0]):
        m = dict(shared)
        m["x"] = xs[b]
        m["xT"] = np.ascontiguousarray(xs[b].T)
        in_maps.append(m)
    return in_maps


def kernel(**inputs):
    in_maps = _prep_inputs(**inputs)
    if "nc" not in _CACHE:
        _CACHE["nc"] = build_program()[0]
    nc = _CACHE["nc"]
    res = run_bass_kernel_spmd(nc, in_maps, core_ids=list(range(8)))
    out = np.stack([np.asarray(r["out"], dtype=np.float32) for r in res.results], axis=0)
    return out
```
